# Optimizing a Trainium2 kernel written in Bass

```python
import math
import jax, jax.numpy as jnp
from jax import lax
import numpy as np

D_MODEL = 2048
BATCH = 16
SEQ = 2048
DEPTH = 1
DEC_BATCH = 4
DEC_SEQ = 4096
PAST_LEN = 128

DILATED_GROUPS = ((128, 1), (512, 4), (2048, 16))
N_DIL_GROUPS = 3
GROUP_HEADS = 4
ATTN_HEAD_DIM = 128
ATTN_HEADS = N_DIL_GROUPS * GROUP_HEADS
ATTN_WIDTH = ATTN_HEADS * ATTN_HEAD_DIM
ATTN_OUT = GROUP_HEADS * ATTN_HEAD_DIM
FNET_GROUPS = 4
FNET_GROUP_DIM = 256
FNET_WIDTH = FNET_GROUPS * FNET_GROUP_DIM
MEM_TOKENS = 256
MEM_HEADS = 4
MEM_HEAD_DIM = 256
MEM_WIDTH = MEM_HEADS * MEM_HEAD_DIM
IN_WIDTH = 3 * ATTN_WIDTH + FNET_WIDTH + MEM_WIDTH
N_BRANCHES = 3
D_FF = 5632
N_BUCKETS = 32
MAX_EXACT = 8
REL_MAX_DIST = 1024
RMS_EPS = 1e-6
NEG_INF = -1e30

kernel_name = "hybrid_dilated_fnet_memory_encoder"


def _rms(x, g):
    xf = x.astype(jnp.float32)
    y = xf * lax.rsqrt(jnp.mean(xf * xf, axis=-1, keepdims=True) + RMS_EPS)
    return (y * g.astype(jnp.float32)).astype(x.dtype)


def _swiglu(h, w_in, w_out):
    a, b = jnp.split(h @ w_in, 2, axis=-1)
    return (jax.nn.silu(a) * b) @ w_out


def _t5_bucket(rel):
    half = N_BUCKETS // 2
    n = jnp.abs(rel)
    nf = jnp.maximum(n, 1).astype(jnp.float32)
    large = MAX_EXACT + (jnp.log(nf / MAX_EXACT) / math.log(REL_MAX_DIST / MAX_EXACT)
                         * (half - MAX_EXACT)).astype(jnp.int32)
    large = jnp.minimum(large, half - 1)
    return jnp.where(rel > 0, half, 0) + jnp.where(n < MAX_EXACT, n, large)


def _dilated_group(q, k, v, window, dil, bias_table):
    B, S, H, E = q.shape
    L = S // dil
    R = window // (2 * dil)
    nblk = -(-L // R)
    Lp = nblk * R

    def sub(t):
        return t.reshape(B, L, dil, H, E).transpose(0, 2, 1, 3, 4)

    qs = jnp.pad(sub(q), ((0, 0), (0, 0), (0, Lp - L), (0, 0), (0, 0))).reshape(B, dil, nblk, R, H, E)

    def key_blocks(t):
        tp = jnp.pad(sub(t), ((0, 0), (0, 0), (R, Lp - L + R), (0, 0), (0, 0)))
        tp = tp.reshape(B, dil, nblk + 2, R, H, E)
        return jnp.concatenate([tp[:, :, :-2], tp[:, :, 1:-1], tp[:, :, 2:]], axis=3)

    kb = key_blocks(k)
    vb = key_blocks(v)
    s_idx = jnp.arange(R)[:, None]
    t_idx = jnp.arange(3 * R)[None, :]
    delta = t_idx - R - s_idx
    bias = bias_table[_t5_bucket(delta * dil)].transpose(2, 0, 1).astype(jnp.float32)
    key_sub = jnp.arange(nblk)[:, None, None] * R - R + t_idx[None]
    valid = (jnp.abs(delta) <= R)[None] & (key_sub >= 0) & (key_sub < L)

    logits = jnp.einsum('bdnqhe,bdnkhe->bdnhqk', qs, kb).astype(jnp.float32) * (E ** -0.5) + bias
    logits = jnp.where(valid[:, None], logits, NEG_INF)
    m = jnp.max(logits, axis=-1, keepdims=True)
    p = jnp.exp(logits - m)
    den = jnp.sum(p, axis=-1, keepdims=True)
    o = jnp.einsum('bdnhqk,bdnkhe->bdnqhe', p / den, vb.astype(jnp.float32))
    lse = (m + jnp.log(den))[..., 0]
    o = o.reshape(B, dil, Lp, H, E)[:, :, :L].transpose(0, 2, 1, 3, 4).reshape(B, S, H, E)
    lse = lse.transpose(0, 1, 2, 4, 3).reshape(B, dil, Lp, H)[:, :, :L].transpose(0, 2, 1, 3).reshape(B, S, H)
    return o, lse


def _dilated_mixture(q, k, v, rel_bias):
    B, S = q.shape[:2]
    outs, lses = [], []
    for g, (window, dil) in enumerate(DILATED_GROUPS):
        o, lse = _dilated_group(q[:, :, g], k[:, :, g], v[:, :, g], window, dil,
                                rel_bias[:, g * GROUP_HEADS:(g + 1) * GROUP_HEADS])
        outs.append(o)
        lses.append(lse)
    alpha = jax.nn.softmax(jnp.stack(lses, axis=0), axis=0)
    o = jnp.sum(alpha[..., None] * jnp.stack(outs, axis=0), axis=0)
    return o.reshape(B, S, ATTN_OUT).astype(q.dtype)


def _fourier(u):
    B, S, _ = u.shape
    ug = u.reshape(B, S, FNET_GROUPS, FNET_GROUP_DIM).astype(jnp.float32)
    f = jnp.fft.fft2(ug, axes=(1, 3), norm="ortho").real
    return f.reshape(B, S, FNET_WIDTH).astype(u.dtype)


def _memory(q_m, mem, mem_norm, w_mem_kv):
    B, S, _ = q_m.shape
    M = mem.shape[1]
    kv = (_rms(mem, mem_norm) @ w_mem_kv).reshape(B, M, 2, MEM_HEADS, MEM_HEAD_DIM)
    q = q_m.reshape(B, S, MEM_HEADS, MEM_HEAD_DIM)
    logits = jnp.einsum('bshe,bmhe->bhsm', q, kv[:, :, 0]).astype(jnp.float32) * (MEM_HEAD_DIM ** -0.5)
    p = jax.nn.softmax(logits, axis=-1)
    o = jnp.einsum('bhsm,bmhe->bshe', p, kv[:, :, 1].astype(jnp.float32))
    return o.reshape(B, S, MEM_WIDTH).astype(q_m.dtype)


def _layer(x, mem, rel_bias, f1_pre, f1_wi, f1_wo, f1_post, mix_pre, mem_norm, w_in, w_mem_kv,
           w_gate, b_gate, w_pa, w_pf, w_pm, w_out, mix_post, f2_pre, f2_wi, f2_wo, f2_post):
    B, S, D = x.shape
    x = x + 0.5 * _rms(_swiglu(_rms(x, f1_pre), f1_wi, f1_wo), f1_post)
    h = _rms(x, mix_pre)
    z = h @ w_in
    qkv_a = z[..., :3 * ATTN_WIDTH].reshape(B, S, 3, N_DIL_GROUPS, GROUP_HEADS, ATTN_HEAD_DIM)
    u_f = z[..., 3 * ATTN_WIDTH:3 * ATTN_WIDTH + FNET_WIDTH]
    q_m = z[..., 3 * ATTN_WIDTH + FNET_WIDTH:]
    a = _dilated_mixture(qkv_a[:, :, 0], qkv_a[:, :, 1], qkv_a[:, :, 2], rel_bias)
    f = _fourier(u_f)
    m = _memory(q_m, mem, mem_norm, w_mem_kv)
    gates = jax.nn.sigmoid((h @ w_gate + b_gate).astype(jnp.float32)).astype(x.dtype).reshape(B, S, N_BRANCHES, D)
    merged = gates[:, :, 0] * (a @ w_pa) + gates[:, :, 1] * (f @ w_pf) + gates[:, :, 2] * (m @ w_pm)
    x = x + _rms(merged @ w_out, mix_post)
    x = x + 0.5 * _rms(_swiglu(_rms(x, f2_pre), f2_wi, f2_wo), f2_post)
    return x


def setup_inputs(seed: int = 0) -> dict:
    key = jax.random.key(seed)
    ks = jax.random.split(key, 32)
    f32 = jnp.float32

    def w(k, shape, fan_in):
        return jax.random.normal(k, shape, f32) * (fan_in ** -0.5)

    def gain(k):
        return 1.0 + 0.01 * jax.random.normal(k, (DEPTH, D_MODEL), f32)

    return {
        "x_prompt": jax.random.normal(ks[0], (BATCH, SEQ, D_MODEL), f32),
        "x_sample": jax.random.normal(ks[1], (DEC_BATCH, DEC_SEQ, D_MODEL), f32),
        "mem_prompt": jax.random.normal(ks[2], (BATCH, MEM_TOKENS, D_MODEL), f32),
        "mem_sample": jax.random.normal(ks[3], (DEC_BATCH, MEM_TOKENS, D_MODEL), f32),
        "rel_bias": 0.1 * jax.random.normal(ks[4], (N_BUCKETS, ATTN_HEADS), f32),
        "ffn1_norm_pre": gain(ks[5]),
        "ffn1_w_in": w(ks[6], (DEPTH, D_MODEL, 2 * D_FF), D_MODEL),
        "ffn1_w_out": w(ks[7], (DEPTH, D_FF, D_MODEL), D_FF),
        "ffn1_norm_post": gain(ks[8]),
        "mix_norm_pre": gain(ks[9]),
        "mem_norm": gain(ks[10]),
        "w_in": w(ks[11], (DEPTH, D_MODEL, IN_WIDTH), D_MODEL),
        "w_mem_kv": w(ks[12], (DEPTH, D_MODEL, 2 * MEM_WIDTH), D_MODEL),
        "w_gate": w(ks[13], (DEPTH, D_MODEL, N_BRANCHES * D_MODEL), D_MODEL),
        "b_gate": 0.01 * jax.random.normal(ks[14], (DEPTH, N_BRANCHES * D_MODEL), f32),
        "w_proj_attn": w(ks[15], (DEPTH, ATTN_OUT, D_MODEL), ATTN_OUT),
        "w_proj_fnet": w(ks[16], (DEPTH, FNET_WIDTH, D_MODEL), FNET_WIDTH),
        "w_proj_mem": w(ks[17], (DEPTH, MEM_WIDTH, D_MODEL), MEM_WIDTH),
        "w_out": w(ks[18], (DEPTH, D_MODEL, D_MODEL), D_MODEL),
        "mix_norm_post": gain(ks[19]),
        "ffn2_norm_pre": gain(ks[20]),
        "ffn2_w_in": w(ks[21], (DEPTH, D_MODEL, 2 * D_FF), D_MODEL),
        "ffn2_w_out": w(ks[22], (DEPTH, D_FF, D_MODEL), D_FF),
        "ffn2_norm_post": gain(ks[23]),
    }


def reference(x_prompt, x_sample, mem_prompt, mem_sample, rel_bias,
              ffn1_norm_pre, ffn1_w_in, ffn1_w_out, ffn1_norm_post,
              mix_norm_pre, mem_norm, w_in, w_mem_kv, w_gate, b_gate,
              w_proj_attn, w_proj_fnet, w_proj_mem, w_out, mix_norm_post,
              ffn2_norm_pre, ffn2_w_in, ffn2_w_out, ffn2_norm_post):
    y_prompt = x_prompt
    y_sample = x_sample
    for l in range(DEPTH):
        lw = (ffn1_norm_pre[l], ffn1_w_in[l], ffn1_w_out[l], ffn1_norm_post[l],
              mix_norm_pre[l], mem_norm[l], w_in[l], w_mem_kv[l], w_gate[l], b_gate[l],
              w_proj_attn[l], w_proj_fnet[l], w_proj_mem[l], w_out[l], mix_norm_post[l],
              ffn2_norm_pre[l], ffn2_w_in[l], ffn2_w_out[l], ffn2_norm_post[l])
        y_prompt = _layer(y_prompt, mem_prompt, rel_bias, *lw)
        y_sample = _layer(y_sample, mem_sample, rel_bias, *lw)
    return (y_prompt, y_sample)
```

```python
import math
import os
import types
from contextlib import ExitStack

import numpy as np
import ml_dtypes

import concourse.bass as bass
import concourse.mybir as mybir
from concourse.bass_utils import run_bass_kernel_spmd

F32 = mybir.dt.float32
BF16 = mybir.dt.bfloat16
AF = mybir.ActivationFunctionType
ALU = mybir.AluOpType

D = 2048
KC = 16
FF = 5632
FC = 44
T = 512
NTOK = 6144
NT = 12
SLOTS = ((0, 4096), (4096, 2048))
GROUPS = ((128, 1), (512, 4), (2048, 16))
NEG = -30000.0
UNIT = 1024
WSLOT = 16384
NRING = 3

V_F1PRE, V_F1POST, V_MIXPRE, V_MIXPOST, V_F2PRE, V_F2POST, V_MEMN, V_BGATE = 0, 16, 32, 48, 64, 80, 96, 112
NVEC = 160


class Sem:
    def __init__(self, h, step):
        self.h, self.step, self.n = h, step, 0


class Buf:
    __slots__ = ("w", "r")

    def __init__(self):
        self.w = None
        self.r = {}


class View:
    def __init__(self, ap, bufs):
        self.ap, self.bufs = ap, bufs


class Eng:
    def __init__(self, name, sem):
        self.name, self.sem, self.q, self.waited = name, sem, [], {}


def _freeze(fn):
    if fn.__closure__ is None:
        return fn
    cells = []
    for c in fn.__closure__:
        try:
            cells.append(types.CellType(c.cell_contents))
        except ValueError:
            cells.append(c)
    return types.FunctionType(fn.__code__, fn.__globals__, fn.__name__, fn.__defaults__, tuple(cells))


class Prog:
    def __init__(self, nc, es):
        self.nc, self.es = nc, es
        self.named = {}
        self.eng = {}
        for name in ("tensor", "vector", "scalar", "gpsimd", "sync"):
            self.eng[name] = Eng(name, self.new_sem(name, 1))
        self.all_dma_sems = []

    def new_sem(self, name, step):
        return Sem(self.es.enter_context(self.nc.semaphore("s_" + name)), step)

    def dma_sem(self, name):
        if name in self.named:
            return self.named[name]
        s = self.new_sem(name, 16)
        self.all_dma_sems.append(s)
        self.named[name] = s
        return s

    def _need(self, e, ev):
        if ev is None:
            return
        sem, val = ev
        if sem is e.sem:
            if e.name in ("tensor", "sync") or sem.n - val >= 3:
                return
        if e.waited.get(sem, 0) >= val:
            return
        e.waited[sem] = val
        e.q.append(("w", sem, val))

    def op(self, engname, fn, reads=(), writes=(), sem=None, flag=True):
        e = self.eng[engname]
        fn = _freeze(fn)
        rb, wb = [], []
        for v in reads:
            rb.extend(v.bufs if isinstance(v, View) else [v])
        for v in writes:
            wb.extend(v.bufs if isinstance(v, View) else [v])
        for b in rb:
            self._need(e, b.w)
        for b in wb:
            self._need(e, b.w)
            for s, val in b.r.items():
                self._need(e, (s, val))
        sem = sem or e.sem
        if flag:
            sem.n += sem.step
            ev = (sem, sem.n)
            e.q.append(("i", fn, sem))
        else:
            ev = (sem, sem.n + sem.step)
            e.q.append(("i", fn, None))
        for b in rb:
            if b.r.get(ev[0], 0) < ev[1]:
                b.r[ev[0]] = ev[1]
        for b in wb:
            b.w = ev
            b.r = {}
        return ev

    def wait_all(self, engname, events):
        e = self.eng[engname]
        for ev in events:
            self._need(e, ev)

    def emit(self, block):
        nc = self.nc

        def run(e, q):
            for it in q:
                if it[0] == "w":
                    e.wait_ge(it[1].h, it[2])
                else:
                    ins = it[1](e)
                    if it[2] is not None:
                        ins.then_inc(it[2].h, it[2].step)

        @block.sync
        def _(e):
            run(e, self.eng["sync"].q)

        @block.scalar
        def _(e):
            run(e, self.eng["scalar"].q)

        @block.vector
        def _(e):
            run(e, self.eng["vector"].q)

        @block.gpsimd
        def _(e):
            run(e, self.eng["gpsimd"].q)

        @block.tensor
        def _(e):
            run(e, self.eng["tensor"].q)


def slab_table():
    specs = []
    idx = {}

    def add(name, kc, nw, ncols, k0c=0):
        idx[name] = []
        for n0 in range(0, ncols, nw):
            idx[name].append(len(specs))
            specs.append((name, kc, nw, n0, k0c))

    add("f1_win", 16, 512, 2 * FF)
    add("f1_wout", 44, 128, D)
    add("w_in", 16, 512, 6656)
    add("w_gate", 16, 512, 3 * D)
    add("w_pa", 4, 512, D)
    add("w_pf", 8, 512, D)
    add("w_pm", 8, 512, D)
    add("w_out", 16, 512, D)
    add("f2_win", 16, 512, 2 * FF)
    add("f2_wout", 44, 128, D)
    add("w_mem", 16, 512, D)
    return specs, idx


WNAME = {"f1_win": "ffn1_w_in", "f1_wout": "ffn1_w_out", "w_in": "w_in", "w_gate": "w_gate", "w_pa": "w_proj_attn",
         "w_pf": "w_proj_fnet", "w_pm": "w_proj_mem", "w_out": "w_out", "f2_win": "ffn2_w_in", "f2_wout": "ffn2_w_out",
         "w_mem": "w_mem_kv"}
WSHAPE = {"ffn1_w_in": (D, 2 * FF), "ffn1_w_out": (FF, D), "w_in": (D, 6656), "w_gate": (D, 3 * D),
          "w_proj_attn": (512, D), "w_proj_fnet": (1024, D), "w_proj_mem": (1024, D), "w_out": (D, D),
          "ffn2_w_in": (D, 2 * FF), "ffn2_w_out": (FF, D), "w_mem_kv": (D, D)}


class _Stop(Exception):
    pass


def build_program(stage=99, debug=False, nta=NT):
    nc = bass.Bass("TRN2", target_bir_lowering=False)
    es = ExitStack()
    specs, sidx = slab_table()
    NSLAB = len(specs)

    def din(name, shape, dt=F32):
        return nc.dram_tensor(name, list(shape), dt, kind="ExternalInput").ap()

    def dscr(name, shape, dt):
        kind = "ExternalOutput" if (debug and (name != "wbf" or stage == 0)) else "Internal"
        return nc.dram_tensor(name, list(shape), dt, kind=kind).ap()

    dbg_n = [0]
    dftn = [0]

    def dump(name, view, shape2, dt=F32):
        if not debug:
            return
        dbg_n[0] += 1
        dd = nc.dram_tensor("dbg_" + name, [128, shape2], dt, kind="ExternalOutput").ap()
        ap = view.ap
        P.op("gpsimd", lambda e: e.dma_start(out=dd[:, :], in_=ap), reads=[view], sem=P.dma_sem(f"dbg{dbg_n[0]}"))

    def checkpoint(k):
        if stage == k:
            raise _Stop()

    xin = din("xin", [NTOK, D])
    mem3 = din("mem3", [3 * 256, D])
    dftA = din("dftA", [2, 4096, 4096], BF16)
    dftB = din("dftB", [2, 2048, 2048], BF16)
    xmask_d = din("xmask", [128, 64])
    vecs_d = din("vecs", [128, NVEC])
    tab_d = din("tab", [128, 12 * 256])
    ident_d = din("ident", [128, 128])
    cs_d = din("cs256", [128, 2 * 512], BF16)
    wd = {n: din(n, WSHAPE[n]) for n in WSHAPE}
    y = nc.dram_tensor("y", [NTOK, D], F32, kind="ExternalOutput").ap()

    wbf = dscr("wbf", [NSLAB, 128, 8192], BF16)
    x1s = dscr("x1s", [NT, 128, KC * T], F32)
    h2s = dscr("h2s", [NT, 128, KC * T], BF16)
    qs = dscr("qs", [12, 128, NTOK], BF16)
    ks = dscr("ks", [12, 128, NTOK], BF16)
    vs = dscr("vs", [NTOK, 1536], BF16)
    us = dscr("us", [8, 128, NTOK], BF16)
    qms = dscr("qms", [8, 128, NTOK], BF16)
    ats = dscr("ats", [4, 128, NTOK], BF16)
    fs = dscr("fs", [8, 128, NTOK], BF16)
    kms = dscr("kms", [3, 128, 8 * 256], BF16)
    vms = dscr("vms", [3, 128, 2 * 1024], BF16)

    CONST_B = 8192
    XT_O = CONST_B
    FO_O = XT_O + 32768
    G_O = FO_O + 32768
    RING_O = G_O + 45056
    STG_O = RING_O + NRING * WSLOT
    STG_B = 24 * 1024
    TOTAL = STG_O + STG_B
    arena = es.enter_context(nc.sbuf_tensor("arena", [128, TOTAL // 4], F32))
    nunits = (TOTAL + UNIT - 1) // UNIT
    ubufs = [Buf() for _ in range(nunits)]

    def V(off, nbytes, dt=F32, pat=None, **kw):
        assert off % 4 == 0 and nbytes % 4 == 0 and off + nbytes <= TOTAL, (off, nbytes)
        ap = arena[:, off // 4:(off + nbytes) // 4]
        if dt == BF16:
            ap = ap.bitcast(BF16)
        if pat:
            ap = ap.rearrange(pat, **kw)
        return View(ap, ubufs[off // UNIT:(off + nbytes - 1) // UNIT + 1])

    psum = [es.enter_context(nc.psum_tensor(f"ps{i}", [128, 512], F32)) for i in range(8)]
    pbuf = [Buf() for _ in range(8)]
    pctr = [0]

    def bank():
        i = pctr[0] % 8
        pctr[0] += 1
        return View(psum[i][:], [pbuf[i]])

    P = Prog(nc, es)

    c_ident = V(0, 512)
    c_ones = V(512, 256, BF16)
    c_vecs = V(768, NVEC * 4)
    c_eps = V(1408, 4)
    c_eps4 = V(1412, 4)
    c_xmask = V(1536, 256)
    c_cs = V(2048, 2048, BF16, "p (k n) -> p k n", k=2)
    c_rstd = V(4096, 2048)
    c_onesf = V(6144, 512)

    s_const = P.dma_sem("const")
    for v, src in ((c_ident, ident_d), (c_vecs, vecs_d), (c_xmask, xmask_d)):
        P.op("sync", lambda e, v=v, src=src: e.dma_start(out=v.ap, in_=src[:, :]), writes=[v], sem=s_const)
    P.op("sync", lambda e: e.dma_start(out=c_cs.ap, in_=cs_d.rearrange("p (k n) -> p k n", k=2)), writes=[c_cs],
         sem=s_const)
    for v in (c_ident, c_vecs, c_xmask, c_cs):
        for b in v.bufs:
            b.w = (s_const, s_const.n)
    P.op("vector", lambda e: e.memset(c_eps.ap, 1e-6), writes=[c_eps])
    P.op("vector", lambda e: e.memset(c_eps4.ap, 4e-6), writes=[c_eps4])
    P.op("vector", lambda e: e.memset(c_onesf.ap, 1.0), writes=[c_onesf])
    P.op("vector", lambda e: e.tensor_copy(out=c_ones.ap, in_=c_onesf.ap), reads=[c_onesf], writes=[c_ones])

    def vec(col, k):
        return c_vecs.ap[:, col + k:col + k + 1]

    stg_off = [STG_O]

    def stg(nbytes, dt=F32, pat=None, **kw):
        o = stg_off[0]
        stg_off[0] += (nbytes + UNIT - 1) // UNIT * UNIT
        assert stg_off[0] <= TOTAL
        return V(o, nbytes, dt, pat, **kw)

    zst = [stg(1024, BF16) for _ in range(4)]
    zsem = [P.dma_sem(f"zst{i}") for i in range(4)]
    zctr = [0]
    sa = [stg(2048) for _ in range(2)]
    sactr = [0]
    tmpf = [stg(2048) for _ in range(3)]
    pTs = [stg(1024, BF16) for _ in range(2)]
    kmT = stg(4096, BF16, "p (c m) -> p c m", c=8)
    vmS = stg(4096, BF16, "p (c f) -> p c f", c=2)

    ring = [V(RING_O + i * WSLOT, WSLOT, BF16) for i in range(NRING)]
    rsem = [P.dma_sem(f"ring{i}") for i in range(NRING)]
    rctr = [0]
    wbuf = [Buf() for _ in range(NSLAB)]

    def ring_slot():
        i = rctr[0] % NRING
        rctr[0] += 1
        return ring[i], rsem[i]

    def load_slab(si):
        _, kc, nw, _, _ = specs[si]
        rv, sem = ring_slot()
        v = View(rv.ap[:, 0:kc * nw].rearrange("p (k n) -> p k n", k=kc), rv.bufs)
        P.op("sync", lambda e: e.dma_start(out=rv.ap[:, 0:kc * nw], in_=wbf[si, :, 0:kc * nw]),
             reads=[wbuf[si]], writes=[rv], sem=sem)
        return v

    evq = [0]

    def evac_eng():
        evq[0] += 1
        return "scalar" if evq[0] % 2 else "vector"

    def evac(dst, src, eng=None):
        eng = eng or evac_eng()
        if eng == "scalar":
            P.op("scalar", lambda e: e.activation(out=dst.ap, in_=src.ap, func=AF.Copy), reads=[src], writes=[dst])
        else:
            P.op("vector", lambda e: e.tensor_copy(out=dst.ap, in_=src.ap), reads=[src], writes=[dst])

    def mm_group(bk_ap, bk, mms, extra_reads=()):
        n = len(mms)
        for i, (l, r) in enumerate(mms):
            P.op("tensor",
                 lambda e, l=l, r=r, i=i: e.matmul(bk_ap, lhsT=l, rhs=r, start=(i == 0), stop=(i == n - 1)),
                 reads=list(extra_reads) if i == 0 else (), writes=[bk] if i == 0 else (), flag=(i == n - 1))

    try:
        cst_f = [V(XT_O, 32768), V(FO_O, 32768)]
        cst_b = [V(G_O, 16384, BF16), V(G_O + 16384, 16384, BF16)]
        s_cf = [P.dma_sem("cf0"), P.dma_sem("cf1")]
        s_cb = [P.dma_sem("cb0"), P.dma_sem("cb1")]
        cast_engs = ("vector", "scalar", "gpsimd")
        for si, (name, kc, nw, n0, _) in enumerate(specs):
            W = wd[WNAME[name]]
            src = W.rearrange("(k p) n -> p k n", p=128)[:, 0:kc, n0:n0 + nw]
            b = si % 2
            ne = kc * nw
            f = View(cst_f[b].ap[:, 0:ne].rearrange("p (k n) -> p k n", k=kc), cst_f[b].bufs)
            fb = View(cst_b[b].ap[:, 0:ne].rearrange("p (k n) -> p k n", k=kc), cst_b[b].bufs)
            P.op("sync", lambda e, f=f, src=src: e.dma_start(out=f.ap, in_=src), writes=[f], sem=s_cf[b])
            ce = cast_engs[si % 3]
            if ce == "scalar":
                P.op("scalar", lambda e, f=f, fb=fb: e.activation(out=fb.ap, in_=f.ap, func=AF.Copy), reads=[f], writes=[fb])
            else:
                P.op(ce, lambda e, f=f, fb=fb: e.tensor_copy(out=fb.ap, in_=f.ap), reads=[f], writes=[fb])
            P.op("gpsimd", lambda e, b=b, ne=ne, si=si: e.dma_start(out=wbf[si, :, 0:ne], in_=cst_b[b].ap[:, 0:ne]),
                 reads=[cst_b[b]], writes=[wbuf[si]], sem=s_cb[b])

        checkpoint(0)
        XT = [V(XT_O + k * 2048, 2048) for k in range(KC)]
        FO = [V(FO_O + k * 2048, 2048) for k in range(KC)]
        H = [V(FO_O + k * 1024, 1024, BF16) for k in range(KC)]
        Hall = V(FO_O, 16384, BF16, "p (k n) -> p k n", k=KC)
        G = [V(G_O + j * 1024, 1024, BF16) for j in range(FC)]
        SQ = V(G_O, 16384, BF16)
        XTall = V(XT_O, 32768, F32, "p (k n) -> p k n", k=KC)
        FOall = V(FO_O, 32768, F32, "p (k n) -> p k n", k=KC)

        def rms_stats(src_all, n, mult=1.0):
            sq = View(SQ.ap[:, 0:KC * n].rearrange("p (k n) -> p k n", k=KC), SQ.bufs)
            P.op("scalar", lambda e: e.activation(out=sq.ap, in_=src_all.ap, func=AF.Square), reads=[src_all], writes=[sq])
            bk = bank()
            mm_group(bk.ap[:, 0:n], bk, [(c_ones.ap, sq.ap[:, k, :]) for k in range(KC)], extra_reads=[sq, c_ones])
            rs = View(c_rstd.ap[:, 0:n], c_rstd.bufs)
            P.op("scalar", lambda e: e.activation(out=rs.ap, in_=bk.ap[:, 0:n], func=AF.Sqrt,
                                                 bias=(c_eps.ap if mult == 1.0 else c_eps4.ap), scale=1.0 / D / (mult * mult)),
                 reads=[bk, c_eps], writes=[rs])
            P.op("vector", lambda e: e.reciprocal(out=rs.ap, in_=rs.ap), reads=[rs], writes=[rs])
            return rs

        def norm_to_bf16(src, dst, col, n, rs):
            for k in range(KC):
                P.op("vector", lambda e, k=k: e.scalar_tensor_tensor(out=dst[k].ap, in0=src[k].ap, scalar=vec(col, k),
                                                                      in1=rs.ap, op0=ALU.mult, op1=ALU.mult),
                     reads=[src[k], rs, c_vecs], writes=[dst[k]])

        def ffn(win_name, wout_name, col_post, half):
            win, wout = sidx[win_name], sidx[wout_name]
            for s in range(11):
                wa = load_slab(win[s])
                wb_ = load_slab(win[s + 11])
                for sub in range(4):
                    j = 4 * s + sub
                    ba, bb = bank(), bank()
                    mm_group(ba.ap, ba, [(wa.ap[:, k, sub * 128:(sub + 1) * 128], H[k].ap) for k in range(KC)],
                             extra_reads=[wa] + H)
                    mm_group(bb.ap, bb, [(wb_.ap[:, k, sub * 128:(sub + 1) * 128], H[k].ap) for k in range(KC)],
                             extra_reads=[wb_] + H)
                    st = sa[sactr[0] % 2]
                    sactr[0] += 1
                    P.op("scalar", lambda e, st=st, ba=ba: e.activation(out=st.ap, in_=ba.ap, func=AF.Silu),
                         reads=[ba], writes=[st])
                    P.op("vector", lambda e, st=st, bb=bb, j=j: e.tensor_tensor(out=G[j].ap, in0=st.ap, in1=bb.ap, op=ALU.mult),
                         reads=[st, bb], writes=[G[j]])
            for c in range(KC):
                w = load_slab(wout[c])
                bk = bank()
                mm_group(bk.ap, bk, [(w.ap[:, j, :], G[j].ap) for j in range(FC)], extra_reads=[w] + G)
                evac(FO[c], bk)
            assert half == 0.5
            rs = rms_stats(FOall, T, mult=0.5)
            for k in range(KC):
                P.op("vector", lambda e, k=k: e.scalar_tensor_tensor(out=FO[k].ap, in0=FO[k].ap, scalar=vec(col_post, k),
                                                                      in1=rs.ap, op0=ALU.mult, op1=ALU.mult),
                     reads=[FO[k], rs], writes=[FO[k]])
                P.op("gpsimd", lambda e, k=k: e.tensor_tensor(out=XT[k].ap, in0=FO[k].ap, in1=XT[k].ap, op=ALU.add),
                     reads=[FO[k], XT[k]], writes=[XT[k]])

        def store(dst_ap, src_view, dbufs, sem):
            P.op("gpsimd", lambda e: e.dma_start(out=dst_ap, in_=src_view.ap), reads=[src_view], writes=dbufs, sem=sem)

        def zstore(bk_ap, bk, dst_ap, dbufs):
            i = zctr[0] % 4
            zctr[0] += 1
            z = zst[i]
            evac(z, View(bk_ap, bk.bufs))
            store(dst_ap, z, dbufs, zsem[i])

        b_x1 = [Buf() for _ in range(NT)]
        b_h2 = [Buf() for _ in range(NT)]
        b_z = [Buf() for _ in range(NT)]
        b_at = [Buf() for _ in range(2)]
        b_f = [Buf() for _ in range(2)]
        b_kv = [Buf() for _ in range(3)]

        win_m = sidx["w_mem"]
        MT_ = [V(XT_O + k * 1024, 1024) for k in range(KC)]
        MTall = V(XT_O, 16384, F32, "p (k n) -> p k n", k=KC)
        HM = [V(XT_O + 16384 + k * 512, 512, BF16) for k in range(KC)]
        for mb in range(3):
            mtm = V(FO_O, 16384, F32, "p (b d) -> p b d", b=2)
            P.op("sync", lambda e, mb=mb: e.dma_start(out=mtm.ap, in_=mem3[mb * 256:(mb + 1) * 256, :].rearrange(
                "(b p) d -> p b d", p=128)), writes=[mtm], sem=P.dma_sem("ld_mtm"))
            for k in range(KC):
                bk = bank()
                for b in range(2):
                    P.op("tensor", lambda e, k=k, b=b, bk=bk: e.transpose(bk.ap[:, b * 128:(b + 1) * 128],
                                                                          mtm.ap[:, b, k * 128:(k + 1) * 128], c_ident.ap),
                         reads=[mtm, c_ident] if b == 0 else (), writes=[bk] if b == 0 else (), flag=(b == 1))
                evac(MT_[k], View(bk.ap[:, 0:256], bk.bufs))
            if mb == 0:
                dump("mtm", V(FO_O, 16384), 4096)
                dump("mt", V(XT_O, 16384), 4096)
            rs = rms_stats(MTall, 256)
            if mb == 0:
                dump("sq", V(G_O, 8192, BF16), 4096, BF16)
                dump("rs", View(c_rstd.ap[:, 0:256], c_rstd.bufs), 256)
            norm_to_bf16(MT_, HM, V_MEMN, 256, rs)
            if mb == 0:
                dump("hm", V(XT_O + 16384, 8192, BF16), 4096, BF16)
                dump("vecs", c_vecs, NVEC)
                dump("ones", c_ones, 128, BF16)
            kst = V(G_O + 16384, 4096, BF16, "p (c m) -> p c m", c=8)
            vst = V(G_O + 20480, 4096, BF16, "p (c f) -> p c f", c=2)
            for s in range(2):
                w = load_slab(win_m[s])
                for sub in range(4):
                    bk = bank()
                    mm_group(bk.ap[:, 0:256], bk, [(w.ap[:, k, sub * 128:(sub + 1) * 128], HM[k].ap) for k in range(KC)],
                             extra_reads=[w] + HM)
                    evac(View(kst.ap[:, s * 4 + sub, :], kst.bufs), View(bk.ap[:, 0:256], bk.bufs))
            for s in range(2):
                w = load_slab(win_m[2 + s])
                for b in range(2):
                    bk = bank()
                    mm_group(bk.ap, bk, [(HM[k].ap[:, b * 128:(b + 1) * 128], w.ap[:, k, :]) for k in range(KC)],
                             extra_reads=[w] + HM)
                    evac(View(vst.ap[:, b, s * 512:(s + 1) * 512], vst.bufs), bk)
            store(kms[mb].rearrange("p (c m) -> p c m", c=8), kst, [b_kv[mb]], P.dma_sem("st_km"))
            store(vms[mb].rearrange("p (c f) -> p c f", c=2), vst, [b_kv[mb]], P.dma_sem("st_vm"))

        checkpoint(1)
        w_in_i = sidx["w_in"]
        for t in range(nta):
            xtm = V(FO_O, 32768, F32, "p (b d) -> p b d", b=4)
            P.op("sync", lambda e, t=t: e.dma_start(out=xtm.ap, in_=xin[t * T:(t + 1) * T, :].rearrange(
                "(b p) d -> p b d", p=128)), writes=[xtm], sem=P.dma_sem("ld_xtm"))
            for k in range(KC):
                bk = bank()
                for b in range(4):
                    P.op("tensor", lambda e, k=k, b=b, bk=bk: e.transpose(bk.ap[:, b * 128:(b + 1) * 128],
                                                                          xtm.ap[:, b, k * 128:(k + 1) * 128], c_ident.ap),
                         reads=[xtm, c_ident] if b == 0 else (), writes=[bk] if b == 0 else (), flag=(b == 3))
                evac(XT[k], bk)
            rs = rms_stats(XTall, T)
            norm_to_bf16(XT, H, V_F1PRE, T, rs)
            ffn("f1_win", "f1_wout", V_F1POST, 0.5)
            store(x1s[t].rearrange("p (k n) -> p k n", k=KC), XTall, [b_x1[t]], P.dma_sem("st_x1"))
            rs = rms_stats(XTall, T)
            norm_to_bf16(XT, H, V_MIXPRE, T, rs)
            store(h2s[t].rearrange("p (k n) -> p k n", k=KC), Hall, [b_h2[t]], P.dma_sem("st_h2"))
            tok = slice(t * T, (t + 1) * T)
            for s in range(13):
                w = load_slab(w_in_i[s])
                if 6 <= s < 9:
                    for b in range(4):
                        bk = bank()
                        mm_group(bk.ap, bk, [(H[k].ap[:, b * 128:(b + 1) * 128], w.ap[:, k, :]) for k in range(KC)],
                                 extra_reads=[w] + H)
                        zstore(bk.ap, bk, vs[t * T + b * 128:t * T + (b + 1) * 128, (s - 6) * 512:(s - 5) * 512], [b_z[t]])
                else:
                    for sub in range(4):
                        bk = bank()
                        mm_group(bk.ap, bk, [(w.ap[:, k, sub * 128:(sub + 1) * 128], H[k].ap) for k in range(KC)],
                                 extra_reads=[w] + H)
                        c = s * 4 + sub
                        if c < 12:
                            dst = qs[c, :, tok]
                        elif c < 24:
                            dst = ks[c - 12, :, tok]
                        elif c < 44:
                            dst = us[c - 36, :, tok]
                        else:
                            dst = qms[c - 44, :, tok]
                        zstore(bk.ap, bk, dst, [b_z[t]])

        checkpoint(2)
        TAB = V(XT_O, 12288, F32, "p (g c) -> p g c", g=12)
        P.op("sync", lambda e: e.dma_start(out=TAB.ap, in_=tab_d.rearrange("p (g c) -> p g c", g=12)), writes=[TAB], sem=P.dma_sem("ld_tab"))
        A_O = XT_O + 12288
        for si_, (off, S) in enumerate(SLOTS):
            tiles = list(range(off // T, (off + S) // T))
            zb = [b_z[t] for t in tiles]
            o = A_O
            QR = [V(o + g * 2 * S, 2 * S, BF16) for g in range(3)]; o += 6 * S
            KR = [V(o + g * 2 * S, 2 * S, BF16) for g in range(3)]; o += 6 * S
            QD = [None] + [V(o + (g - 1) * 2 * S, 2 * S, BF16) for g in (1, 2)]; o += 4 * S
            KD = [None] + [V(o + (g - 1) * 2 * S, 2 * S, BF16) for g in (1, 2)]; o += 4 * S
            VV = [V(o + g * 2 * S, 2 * S, BF16) for g in range(3)]; o += 6 * S
            NUM = V(o, 4 * S); o += 4 * S
            DEN = V(o, 4 * S); o += 4 * S
            LG = [V(o + i * 1024, 1024) for i in range(2)]; o += 2048
            PT = [V(o + i * 512, 512, BF16) for i in range(2)]; o += 2048
            assert o <= RING_O + NRING * WSLOT, o
            AOUT = V(RING_O, 2 * S, BF16) if 2 * S <= NRING * WSLOT and o <= RING_O else None
            if AOUT is None:
                AOUT = QR[0]
            lctr = 0
            for hg in range(4):
                for g, (win, d) in enumerate(GROUPS):
                    c = g * 4 + hg
                    P.op("sync", lambda e, g=g, c=c: e.dma_start(out=QR[g].ap, in_=qs[c, :, off:off + S]),
                         reads=zb, writes=[QR[g]], sem=P.dma_sem(f"ld_qr{g}_{si_}"))
                    P.op("sync", lambda e, g=g, c=c: e.dma_start(out=KR[g].ap, in_=ks[c, :, off:off + S]),
                         reads=zb, writes=[KR[g]], sem=P.dma_sem(f"ld_kr{g}_{si_}"))
                    L = S // d
                    nch = L // 128
                    vsrc = vs[off:off + S, c * 128:(c + 1) * 128].rearrange("(cc l r) e -> l r cc e", r=d, l=128)
                    vdst = VV[g].ap.rearrange("p (r cc e) -> p r cc e", r=d, cc=nch)
                    ccs = max(1, 1024 // 128 // 1)
                    for r in range(d):
                        for cc0 in range(0, nch, ccs):
                            cc1 = min(nch, cc0 + ccs)
                            P.op("sync", lambda e, vdst=vdst, vsrc=vsrc, r=r, cc0=cc0, cc1=cc1: e.dma_start(
                                out=vdst[:, r, cc0:cc1, :], in_=vsrc[:, r, cc0:cc1, :]),
                                reads=zb, writes=[VV[g]], sem=P.dma_sem(f"ld_vv{g}_{si_}"))
                    if d > 1:
                        P.op("gpsimd", lambda e, g=g, d=d: e.tensor_copy(out=QD[g].ap.rearrange("p (r l) -> p r l", r=d),
                                                                         in_=QR[g].ap.rearrange("p (l r) -> p r l", r=d)),
                             reads=[QR[g]], writes=[QD[g]])
                        P.op("gpsimd", lambda e, g=g, d=d: e.tensor_copy(out=KD[g].ap.rearrange("p (r l) -> p r l", r=d),
                                                                         in_=KR[g].ap.rearrange("p (l r) -> p r l", r=d)),
                             reads=[KR[g]], writes=[KD[g]])
                P.op("gpsimd", lambda e: e.memset(NUM.ap, 0.0), writes=[NUM])
                P.op("gpsimd", lambda e: e.memset(DEN.ap, 0.0), writes=[DEN])
                for g, (win, d) in enumerate(GROUPS):
                    L = S // d
                    nch = L // 128
                    Lh = 2048 // d
                    qd = QR[g] if d == 1 else QD[g]
                    kd = KR[g] if d == 1 else KD[g]
                    vdst = VV[g].ap.rearrange("p (r cc e) -> p r cc e", r=d, cc=nch)
                    numv = NUM.ap.rearrange("p (l r) -> p r l", r=d)
                    denv = DEN.ap.rearrange("p (l r) -> p r l", r=d)
                    for r in range(d):
                        for cc in range(nch):
                            a = cc * 128
                            qlo, qhi = max(0, a - 64), min(L, a + 192)
                            c0, c1 = qlo - (a - 64), qhi - (a - 64)
                            nq = qhi - qlo
                            bs = bank()
                            kap = kd.ap[:, r * L + a:r * L + a + 128]
                            qap = qd.ap[:, r * L + qlo:r * L + qhi]
                            mm_group(bs.ap[:, 0:nq], bs, [(kap, qap)], extra_reads=[kd, qd])
                            lg = LG[lctr % 2]
                            pt = PT[lctr % 2]
                            lctr += 1
                            gh = g * 4 + hg
                            P.op("vector", lambda e, lg=lg, bs=bs, nq=nq, gh=gh, c0=c0, c1=c1: e.scalar_tensor_tensor(
                                out=lg.ap[:, 0:nq], in0=bs.ap[:, 0:nq], scalar=128 ** -0.5, in1=TAB.ap[:, gh, c0:c1],
                                op0=ALU.mult, op1=ALU.add), reads=[bs, TAB], writes=[lg])
                            if S == 4096 and (a + 128 == Lh or a == Lh):
                                x0 = (192 - c0) if a + 128 == Lh else 0
                                P.op("vector", lambda e, lg=lg, x0=x0: e.tensor_tensor(
                                    out=lg.ap[:, x0:x0 + 64], in0=lg.ap[:, x0:x0 + 64], in1=c_xmask.ap, op=ALU.add),
                                    reads=[lg, c_xmask], writes=[lg])
                            P.op("scalar", lambda e, lg=lg, pt=pt, nq=nq: e.activation(out=pt.ap[:, 0:nq], in_=lg.ap[:, 0:nq],
                                                                                         func=AF.Exp), reads=[lg], writes=[pt])
                            bo, bd = bank(), bank()
                            mm_group(bo.ap[:, 0:nq], bo, [(vdst[:, r, cc, :], pt.ap[:, 0:nq])], extra_reads=[VV[g], pt])
                            mm_group(bd.ap[:, 0:nq], bd, [(c_ones.ap, pt.ap[:, 0:nq])], extra_reads=[pt])
                            P.op("vector", lambda e, numv=numv, r=r, qlo=qlo, qhi=qhi, bo=bo, nq=nq: e.tensor_tensor(
                                out=numv[:, r, qlo:qhi], in0=numv[:, r, qlo:qhi], in1=bo.ap[:, 0:nq], op=ALU.add),
                                reads=[bo, NUM], writes=[NUM])
                            P.op("vector", lambda e, denv=denv, r=r, qlo=qlo, qhi=qhi, bd=bd, nq=nq: e.tensor_tensor(
                                out=denv[:, r, qlo:qhi], in0=denv[:, r, qlo:qhi], in1=bd.ap[:, 0:nq], op=ALU.add),
                                reads=[bd, DEN], writes=[DEN])
                P.op("vector", lambda e: e.reciprocal(out=DEN.ap, in_=DEN.ap), reads=[DEN], writes=[DEN])
                P.op("vector", lambda e: e.tensor_tensor(out=AOUT.ap, in0=NUM.ap, in1=DEN.ap, op=ALU.mult),
                     reads=[NUM, DEN], writes=[AOUT])
                store(ats[hg, :, off:off + S], AOUT, [b_at[si_]], P.dma_sem(f"st_at{si_}"))

        checkpoint(3)
        for si_, (off, S) in enumerate(SLOTS):
            tiles = list(range(off // T, (off + S) // T))
            zb = [b_z[t] for t in tiles]
            nb = S // 128
            dft = dftA if S == 4096 else dftB
            for hf in range(2):
                UT = V(XT_O, 8 * S, BF16, "p (c s) -> p c s", c=4)
                UCS = V(XT_O + 8 * S, 16 * S, BF16, "p (b g n) -> p b g n", b=nb, g=2)
                assert XT_O + 24 * S <= RING_O
                for cc in range(4):
                    P.op("sync", lambda e, cc=cc, hf=hf: e.dma_start(out=UT.ap[:, cc, :], in_=us[hf * 4 + cc, :, off:off + S]),
                         reads=zb, writes=[UT], sem=P.dma_sem(f"ld_ut{si_}"))
                for b in range(nb):
                    for gg in range(2):
                        bk = bank()
                        mm_group(bk.ap, bk, [(UT.ap[:, gg * 2 + kk, b * 128:(b + 1) * 128], c_cs.ap[:, kk, :]) for kk in range(2)],
                                 extra_reads=[UT, c_cs])
                        evac(View(UCS.ap[:, b, gg, :], UCS.bufs), bk)
                checkpoint(3.5)
                for st_ in range(S // T):
                    bks = [bank() for _ in range(4)]
                    for b0 in range(0, nb, 8):
                        rv, sem = ring_slot()
                        rvv = View(rv.ap.rearrange("p (b c n) -> p b c n", b=8, c=2), rv.bufs)
                        for part in range(2):
                            src = dft[part, b0 * 128:(b0 + 8) * 128, st_ * T:(st_ + 1) * T].rearrange("(b p) n -> p b n", p=128)
                            P.op("sync", lambda e, rvv=rvv, src=src, part=part: e.dma_start(out=rvv.ap[:, :, part, :], in_=src),
                                 writes=[rvv], sem=sem)
                        for bi in range(8):
                            b = b0 + bi
                            for q4 in range(4):
                                gg, hh = q4 // 2, q4 % 2
                                for part in range(2):
                                    first = (b == 0 and part == 0)
                                    last = (b == nb - 1 and part == 1)
                                    P.op("tensor", lambda e, q4=q4, gg=gg, hh=hh, part=part, b=b, bi=bi, rvv=rvv, first=first, last=last, bks=bks:
                                         e.matmul(bks[q4].ap, lhsT=UCS.ap[:, b, gg, part * 256 + hh * 128:part * 256 + (hh + 1) * 128],
                                                  rhs=rvv.ap[:, bi, part, :], start=first, stop=last),
                                         reads=[rvv, UCS],
                                         writes=[bks[q4]], flag=(last or (bi == 7 and part == 1 and q4 == 3)))
                    for q4 in range(4):
                        zstore(bks[q4].ap, bks[q4], fs[hf * 4 + q4, :, off + st_ * T:off + (st_ + 1) * T], [b_f[si_]])
                    dftn[0] += 1
                    if dftn[0] >= int(os.environ.get("DFTN", "1")):
                        checkpoint(3.75)

        checkpoint(4)
        MRG = [V(G_O + k * 1024, 1024, BF16) for k in range(KC)]
        AT = V(G_O + 16384, 4096, BF16, "p (c n) -> p c n", c=4)
        FT = V(G_O + 20480, 8192, BF16, "p (c n) -> p c n", c=8)
        MT = V(G_O + 28672, 8192, BF16, "p (c n) -> p c n", c=8)
        QMT = V(G_O + 36864, 8192, BF16, "p (c n) -> p c n", c=8)
        H2all = V(FO_O, 16384, BF16, "p (k n) -> p k n", k=KC)
        gate_i, pa_i, pf_i, pm_i, wo_i = sidx["w_gate"], sidx["w_pa"], sidx["w_pf"], sidx["w_pm"], sidx["w_out"]
        ytm = [V(FO_O + i * 8192, 8192) for i in range(2)]
        ysem = [P.dma_sem("y0"), P.dma_sem("y1")]
        for t in range(NT):
            slot = 0 if t < 8 else 1
            mb = t // 4
            tok = slice(t * T, (t + 1) * T)
            P.op("sync", lambda e, t=t: e.dma_start(out=XTall.ap, in_=x1s[t].rearrange("p (k n) -> p k n", k=KC)),
                 reads=[b_x1[t]], writes=[XTall], sem=P.dma_sem("ld_pb_x"))
            P.op("sync", lambda e, t=t: e.dma_start(out=H2all.ap, in_=h2s[t].rearrange("p (k n) -> p k n", k=KC)),
                 reads=[b_h2[t]], writes=[H2all], sem=P.dma_sem("ld_pb_h2"))
            P.op("sync", lambda e, tok=tok: e.dma_start(out=AT.ap, in_=ats[:, :, tok].rearrange("c p n -> p c n")),
                 reads=[b_at[slot]], writes=[AT], sem=P.dma_sem("ld_pb_at"))
            P.op("sync", lambda e, tok=tok: e.dma_start(out=FT.ap, in_=fs[:, :, tok].rearrange("c p n -> p c n")),
                 reads=[b_f[slot]], writes=[FT], sem=P.dma_sem("ld_pb_ft"))
            P.op("sync", lambda e, tok=tok: e.dma_start(out=QMT.ap, in_=qms[:, :, tok].rearrange("c p n -> p c n")),
                 reads=[b_z[t]], writes=[QMT], sem=P.dma_sem("ld_pb_qm"))
            P.op("sync", lambda e, mb=mb: e.dma_start(out=kmT.ap, in_=kms[mb].rearrange("p (c m) -> p c m", c=8)),
                 reads=[b_kv[mb]], writes=[kmT], sem=P.dma_sem("ld_km"))
            P.op("sync", lambda e, mb=mb: e.dma_start(out=vmS.ap, in_=vms[mb].rearrange("p (c f) -> p c f", c=2)),
                 reads=[b_kv[mb]], writes=[vmS], sem=P.dma_sem("ld_vm"))
            for hm in range(4):
                for mc in range(2):
                    bk = bank()
                    mm_group(bk.ap, bk, [(kmT.ap[:, hm * 2 + ec, mc * 128:(mc + 1) * 128], QMT.ap[:, hm * 2 + ec, :]) for ec in range(2)],
                             extra_reads=[kmT, QMT])
                    P.op("scalar", lambda e, mc=mc, bk=bk: e.activation(out=pTs[mc].ap, in_=bk.ap, func=AF.Exp, scale=1.0 / 16.0),
                         reads=[bk], writes=[pTs[mc]])
                bd = bank()
                mm_group(bd.ap, bd, [(c_ones.ap, pTs[mc].ap) for mc in range(2)], extra_reads=pTs)
                rec = tmpf[0]
                P.op("vector", lambda e, bd=bd: e.reciprocal(out=rec.ap, in_=bd.ap), reads=[bd], writes=[rec])
                for ec in range(2):
                    bo = bank()
                    mm_group(bo.ap, bo, [(vmS.ap[:, mc, hm * 256 + ec * 128:hm * 256 + (ec + 1) * 128], pTs[mc].ap) for mc in range(2)],
                             extra_reads=pTs + [vmS])
                    P.op("vector", lambda e, bo=bo, hm=hm, ec=ec: e.tensor_tensor(out=MT.ap[:, hm * 2 + ec, :], in0=bo.ap, in1=rec.ap,
                                                                                   op=ALU.mult), reads=[bo, rec], writes=[MT])
            if t == 0:
                dump("mtq", MT, 4096, BF16)
            srcs = [(AT, 4), (FT, 8), (MT, 8)]
            wpi = [pa_i, pf_i, pm_i]
            t1ctr = 0
            for cg in range(4):
                accs = [V(FO_O + 16384 + sub * 2048, 2048) for sub in range(4)]
                for i in range(3):
                    wg = load_slab(gate_i[i * 4 + cg])
                    wp = load_slab(wpi[i][cg])
                    sv, nk = srcs[i]
                    for sub in range(4):
                        c = cg * 4 + sub
                        acc = accs[sub]
                        bp, bg = bank(), bank()
                        mm_group(bp.ap, bp, [(wp.ap[:, k, sub * 128:(sub + 1) * 128], sv.ap[:, k, :]) for k in range(nk)],
                                 extra_reads=[wp, sv])
                        mm_group(bg.ap, bg, [(wg.ap[:, k, sub * 128:(sub + 1) * 128], H2all.ap[:, k, :]) for k in range(KC)],
                                 extra_reads=[wg, H2all])
                        st = sa[sactr[0] % 2]
                        sactr[0] += 1
                        P.op("scalar", lambda e, st=st, bg=bg, i=i, c=c: e.activation(out=st.ap, in_=bg.ap, func=AF.Sigmoid,
                                                                                       bias=vec(V_BGATE + i * 16, c)),
                             reads=[bg, c_vecs], writes=[st])
                        if i == 0:
                            P.op("vector", lambda e, st=st, bp=bp, acc=acc: e.tensor_tensor(out=acc.ap, in0=st.ap, in1=bp.ap, op=ALU.mult),
                                 reads=[st, bp], writes=[acc])
                        else:
                            t1 = tmpf[1 + t1ctr % 2]
                            t1ctr += 1
                            P.op("vector", lambda e, st=st, bp=bp, t1=t1: e.tensor_tensor(out=t1.ap, in0=st.ap, in1=bp.ap, op=ALU.mult),
                                 reads=[st, bp], writes=[t1])
                            dstv = acc if i == 1 else MRG[c]
                            P.op("gpsimd", lambda e, t1=t1, dstv=dstv, acc=acc: e.tensor_tensor(out=dstv.ap, in0=acc.ap, in1=t1.ap, op=ALU.add),
                                 reads=[acc, t1], writes=[dstv])
            if t == 0:
                dump("mrg", V(G_O, 16384, BF16), 8192, BF16)
            for s in range(4):
                w = load_slab(wo_i[s])
                for sub in range(4):
                    bk = bank()
                    mm_group(bk.ap, bk, [(w.ap[:, k, sub * 128:(sub + 1) * 128], MRG[k].ap) for k in range(KC)],
                             extra_reads=[w] + MRG)
                    evac(FO[s * 4 + sub], bk)
            rs = rms_stats(FOall, T)
            for k in range(KC):
                P.op("vector", lambda e, k=k: e.scalar_tensor_tensor(out=FO[k].ap, in0=FO[k].ap, scalar=vec(V_MIXPOST, k),
                                                                      in1=rs.ap, op0=ALU.mult, op1=ALU.mult),
                     reads=[FO[k], rs], writes=[FO[k]])
                P.op("gpsimd", lambda e, k=k: e.tensor_tensor(out=XT[k].ap, in0=FO[k].ap, in1=XT[k].ap, op=ALU.add),
                     reads=[FO[k], XT[k]], writes=[XT[k]])
            if t == 0:
                dump("x2", XTall, 8192)
            rs = rms_stats(XTall, T)
            norm_to_bf16(XT, H, V_F2PRE, T, rs)
            ffn("f2_win", "f2_wout", V_F2POST, 0.5)
            if t == 0:
                dump("x3", XTall, 8192)
            for b in range(4):
                yv = ytm[b % 2]
                for kq in range(4):
                    bk = bank()
                    for kk in range(4):
                        k = kq * 4 + kk
                        P.op("tensor", lambda e, k=k, kk=kk, b=b, bk=bk: e.transpose(bk.ap[:, kk * 128:(kk + 1) * 128],
                                                                                   XT[k].ap[:, b * 128:(b + 1) * 128], c_ident.ap),
                             reads=XT[kq * 4:kq * 4 + 4] if kk == 0 else (), writes=[bk] if kk == 0 else (), flag=(kk == 3))
                    evac(View(yv.ap[:, kq * 512:(kq + 1) * 512], yv.bufs), bk)
                P.op("gpsimd", lambda e, yv=yv, t=t, b=b: e.dma_start(out=y[t * T + b * 128:t * T + (b + 1) * 128, :], in_=yv.ap),
                     reads=[yv], sem=ysem[b % 2])
    except _Stop:
        pass

    P.wait_all("gpsimd", [(s, s.n) for s in P.all_dma_sems if s.n > 0])
    P.wait_all("sync", [(P.eng[n].sem, P.eng[n].sem.n) for n in ("tensor", "vector", "scalar", "gpsimd") if P.eng[n].sem.n > 0])

    block = es.enter_context(nc.Block())
    P.emit(block)
    es.close()
    return nc


def _t5_bucket_np(rel):
    n = np.abs(rel)
    nf = np.maximum(n, 1).astype(np.float32)
    large = 8 + (np.log(nf / np.float32(8)) / np.float32(math.log(1024 / 8)) * np.float32(8)).astype(np.int32)
    large = np.minimum(large, 15)
    return np.where(rel > 0, 16, 0) + np.where(n < 8, n, large)


def _dft_mats(n):
    k = np.arange(n, dtype=np.int64)
    idx = (np.outer(k, k) % n).astype(np.int32)
    ang = np.arange(n, dtype=np.float64) * (2.0 * np.pi / n)
    return np.cos(ang).astype(np.float32)[idx], np.sin(ang).astype(np.float32)[idx]


_CONST_CACHE = {}


def _constants():
    if _CONST_CACHE:
        return _CONST_CACHE
    bf = ml_dtypes.bfloat16
    c256, s256 = _dft_mats(256)
    cs = np.concatenate([c256, s256], axis=1) / 16.0
    cs = cs.reshape(2, 128, 512).transpose(1, 0, 2).reshape(128, 1024)
    _CONST_CACHE["cs256"] = np.ascontiguousarray(cs.astype(np.float32).astype(bf))
    c2, s2 = _dft_mats(2048)
    c4, s4 = _dft_mats(4096)
    dB = (np.stack([c2, -s2]) * np.float32(1.0 / math.sqrt(2048))).astype(bf)
    dA_s = (np.stack([c4, -s4]) * np.float32(1.0 / math.sqrt(4096))).astype(bf)
    dA_p = np.zeros((2, 4096, 4096), dtype=bf)
    dA_p[:, :2048, :2048] = dB
    dA_p[:, 2048:, 2048:] = dB
    _CONST_CACHE["dftB"] = dB
    _CONST_CACHE["dftA_s"] = dA_s
    _CONST_CACHE["dftA_p"] = dA_p
    _CONST_CACHE["ident"] = np.eye(128, dtype=np.float32)
    j = np.arange(128)[:, None]
    c = np.arange(256)[None, :]
    delta = j + 64 - c
    _CONST_CACHE["valid"] = np.abs(delta) <= 64
    _CONST_CACHE["bucket"] = [_t5_bucket_np((delta * d).astype(np.int32)) for (_, d) in GROUPS]
    return _CONST_CACHE


_NC_CACHE = {}


def _layout_vec(v):
    return np.ascontiguousarray(np.asarray(v, dtype=np.float32).reshape(-1, 128).T)


def make_in_maps(inputs, cores=range(8)):
    C = _constants()
    f32 = np.float32
    xp = np.asarray(inputs["x_prompt"], dtype=f32)
    xs = np.asarray(inputs["x_sample"], dtype=f32)
    mp = np.asarray(inputs["mem_prompt"], dtype=f32)
    ms = np.asarray(inputs["mem_sample"], dtype=f32)
    rb = np.asarray(inputs["rel_bias"], dtype=f32)
    vecs = np.zeros((128, NVEC), dtype=f32)
    for col, name in ((V_F1PRE, "ffn1_norm_pre"), (V_F1POST, "ffn1_norm_post"), (V_MIXPRE, "mix_norm_pre"),
                      (V_MIXPOST, "mix_norm_post"), (V_F2PRE, "ffn2_norm_pre"), (V_F2POST, "ffn2_norm_post"),
                      (V_MEMN, "mem_norm")):
        vecs[:, col:col + 16] = _layout_vec(inputs[name][0])
    vecs[:, V_BGATE:V_BGATE + 48] = _layout_vec(inputs["b_gate"][0])
    tab = np.empty((12, 128, 256), dtype=f32)
    for g in range(3):
        for h in range(4):
            tab[g * 4 + h] = np.where(C["valid"], rb[C["bucket"][g], g * 4 + h], f32(NEG))
    tab = np.ascontiguousarray(tab.transpose(1, 0, 2).reshape(128, 12 * 256))
    weights = {n: np.ascontiguousarray(np.asarray(inputs[n], dtype=f32)[0]) for n in WSHAPE}
    maps = []
    for c in cores:
        if c < 4:
            xin = np.concatenate([xs[c], xp[c]], axis=0)
            mem3 = np.concatenate([ms[c], ms[c], mp[c]], axis=0)
            dA = C["dftA_s"]
            xm = np.zeros((128, 64), dtype=f32)
        else:
            p0 = 4 + 3 * (c - 4)
            xin = np.concatenate([xp[p0], xp[p0 + 1], xp[p0 + 2]], axis=0)
            mem3 = np.concatenate([mp[p0], mp[p0 + 1], mp[p0 + 2]], axis=0)
            dA = C["dftA_p"]
            xm = np.full((128, 64), NEG, dtype=f32)
        m = {"xin": np.ascontiguousarray(xin), "mem3": np.ascontiguousarray(mem3), "dftA": dA, "dftB": C["dftB"],
             "xmask": xm, "vecs": vecs, "tab": tab, "ident": C["ident"], "cs256": C["cs256"]}
        m.update(weights)
        maps.append(m)
    return maps


def assemble(results, cores=range(8)):
    yp = np.zeros((16, 2048, D), dtype=np.float32)
    ys = np.zeros((4, 4096, D), dtype=np.float32)
    for r, c in zip(results, cores):
        yy = np.asarray(r["y"], dtype=np.float32)
        if c < 4:
            ys[c] = yy[:4096]
            yp[c] = yy[4096:]
        else:
            p0 = 4 + 3 * (c - 4)
            for i in range(3):
                yp[p0 + i] = yy[i * 2048:(i + 1) * 2048]
    return yp, ys


def kernel(**inputs):
    if "nc" not in _NC_CACHE:
        _NC_CACHE["nc"] = build_program()
    nc = _NC_CACHE["nc"]
    in_maps = make_in_maps(inputs)
    res = run_bass_kernel_spmd(nc, in_maps, core_ids=list(range(8)))
    return assemble(res.results)
```
